# Optimizing a Trainium2 kernel written in Bass

```python
import math
import jax, jax.numpy as jnp
from jax import lax
import numpy as np

D_MODEL = 4096
BATCH = 4
SEQ = 4096
DEPTH = 2

HEAD_DIM = 128
POOL_WINDOWS = (2, 4, 8, 16)
POOL_GROUPS = 4
POOL_GROUP_DIM = D_MODEL // 16
POOL_DIM = POOL_GROUPS * POOL_GROUP_DIM
ATTN_DIM = (D_MODEL - POOL_DIM) // 2
FOX_HEADS = ATTN_DIM // HEAD_DIM
GDN_DIM = D_MODEL - POOL_DIM - ATTN_DIM
GDN_HEADS = GDN_DIM // HEAD_DIM
MIX_DIM = POOL_DIM + ATTN_DIM + GDN_DIM
Q_BLOCK = 128
GDN_CHUNK = 64
GDN_CONV = 4
FFN_DIM = 11008
FFN_CONV = 3
EPS = 1e-6
IN_SIZES = (POOL_DIM, 3 * ATTN_DIM, FOX_HEADS, 3 * GDN_DIM, GDN_DIM, GDN_HEADS, GDN_HEADS)
IN_DIM = sum(IN_SIZES)

kernel_name = "hybrid_pool_fox_gdn_convffn"


def rms_norm(x, gain):
    xf = x.astype(jnp.float32)
    xf = xf * lax.rsqrt(jnp.mean(xf * xf, axis=-1, keepdims=True) + EPS)
    return (xf * gain.astype(jnp.float32)).astype(x.dtype)


def l2_normalize(x):
    return x * lax.rsqrt(jnp.sum(x * x, axis=-1, keepdims=True) + EPS)


def causal_depthwise_conv(x, w):
    k_width = w.shape[0]
    t = x.shape[1]
    xp = jnp.pad(x, ((0, 0), (k_width - 1, 0), (0, 0)))
    return sum(xp[:, i:i + t] * w[i] for i in range(k_width))


def split_columns(h):
    outs = []
    start = 0
    for size in IN_SIZES:
        outs.append(h[..., start:start + size])
        start += size
    return outs


def multiscale_pool_mixer(v, w_group, scale):
    b, t, _ = v.shape
    vf = v.astype(jnp.float32).reshape(b, t, POOL_GROUPS, POOL_GROUP_DIM)
    cs = jnp.cumsum(vf, axis=1)
    pos = jnp.arange(1, t + 1, dtype=jnp.float32)
    outs = []
    for gi, win in enumerate(POOL_WINDOWS):
        c = cs[:, :, gi]
        shifted = jnp.pad(c, ((0, 0), (win, 0), (0, 0)))[:, :t]
        mean = (c - shifted) / jnp.minimum(pos, float(win))[None, :, None]
        outs.append(mean - vf[:, :, gi])
    pooled = jnp.stack(outs, axis=2).astype(v.dtype)
    y = jnp.einsum('btgc,gce->btge', pooled, w_group).reshape(b, t, POOL_DIM)
    return y * scale


def forgetting_attention(q, k, v, log_f):
    b, t, h, d = q.shape
    scale = 1.0 / math.sqrt(d)
    cum = jnp.transpose(jnp.cumsum(log_f, axis=1), (0, 2, 1))
    outs = []
    for blk in range(t // Q_BLOCK):
        t0, t1 = blk * Q_BLOCK, (blk + 1) * Q_BLOCK
        qb, kb, vb = q[:, t0:t1], k[:, :t1], v[:, :t1]
        s = jnp.einsum('bqhd,bkhd->bhqk', qb, kb).astype(jnp.float32) * scale
        s = s + (cum[:, :, t0:t1, None] - cum[:, :, None, :t1])
        mask = jnp.arange(t0, t1)[:, None] >= jnp.arange(t1)[None, :]
        s = jnp.where(mask, s, -jnp.inf)
        p = jax.nn.softmax(s, axis=-1).astype(v.dtype)
        outs.append(jnp.einsum('bhqk,bkhd->bqhd', p, vb))
    return jnp.concatenate(outs, axis=1)


def gated_delta_rule(q, k, v, g, beta):
    b, t, h, d = q.shape
    n = t // GDN_CHUNK
    q = l2_normalize(q) * (d ** -0.5)
    k = l2_normalize(k)

    def chunks(a):
        a = a.reshape((b, n, GDN_CHUNK, h) + a.shape[3:])
        return jnp.moveaxis(a, 3, 1)

    q, k, v = chunks(q), chunks(k), chunks(v)
    beta = chunks(beta)
    g = jnp.cumsum(chunks(g), axis=-1)
    idx = jnp.arange(GDN_CHUNK)
    causal = idx[:, None] >= idx[None, :]
    strict = idx[:, None] > idx[None, :]
    decay = jnp.exp(jnp.where(causal, g[..., :, None] - g[..., None, :], -jnp.inf))
    k_beta = k * beta[..., None]
    a_mat = jnp.where(strict, jnp.einsum('bhncd,bhnsd->bhncs', k_beta, k) * decay, 0.0)
    rhs = jnp.concatenate([v * beta[..., None], k_beta * jnp.exp(g)[..., None]], axis=-1)
    eye = jnp.eye(GDN_CHUNK, dtype=jnp.float32)
    sol = lax.linalg.triangular_solve(a_mat + eye, rhs, left_side=True, lower=True,
                                      unit_diagonal=True)
    u, w = sol[..., :d], sol[..., d:]
    intra = jnp.where(causal, jnp.einsum('bhncd,bhnsd->bhncs', q, k) * decay, 0.0)
    g_last = g[..., -1]
    k_dec = k * jnp.exp(g_last[..., None] - g)[..., None]
    q_dec = q * jnp.exp(g)[..., None]

    def step(state, inp):
        q_n, k_n, u_n, w_n, intra_n, gl_n = inp
        v_new = u_n - jnp.einsum('bhcd,bhde->bhce', w_n, state)
        o = jnp.einsum('bhcd,bhde->bhce', q_n, state) + jnp.einsum('bhcs,bhse->bhce', intra_n, v_new)
        state = state * jnp.exp(gl_n)[..., None, None] + jnp.einsum('bhcd,bhce->bhde', k_n, v_new)
        return state, o

    xs = tuple(jnp.moveaxis(a, 2, 0) for a in (q_dec, k_dec, u, w, intra, g_last))
    state0 = jnp.zeros((b, h, d, v.shape[-1]), jnp.float32)
    _, o = lax.scan(step, state0, xs)
    return jnp.transpose(o, (1, 0, 3, 2, 4)).reshape(b, t, h, -1)


def setup_inputs(seed: int = 0) -> dict:
    key = jax.random.key(seed)
    ks = jax.random.split(key, 16)
    f32 = jnp.float32
    x = jax.random.normal(ks[0], (BATCH, SEQ, D_MODEL), f32)
    norm_mix_gain = 1.0 + 0.05 * jax.random.normal(ks[1], (DEPTH, D_MODEL), f32)
    w_in = jax.random.normal(ks[2], (DEPTH, D_MODEL, IN_DIM), f32) * D_MODEL ** -0.5
    pool_w = jax.random.normal(ks[3], (DEPTH, POOL_GROUPS, POOL_GROUP_DIM, POOL_GROUP_DIM), f32) * POOL_GROUP_DIM ** -0.5
    pool_scale = 1.0 + 0.1 * jax.random.normal(ks[4], (DEPTH, POOL_DIM), f32)
    fox_f_bias = 2.0 + 0.5 * jax.random.normal(ks[5], (DEPTH, FOX_HEADS), f32)
    gdn_conv_w = jax.random.normal(ks[6], (DEPTH, GDN_CONV, 3 * GDN_DIM), f32) * GDN_CONV ** -0.5
    gdn_A_log = jnp.log(jax.random.uniform(ks[7], (DEPTH, GDN_HEADS), f32, 1.0, 16.0))
    dt = jnp.exp(jax.random.uniform(ks[8], (DEPTH, GDN_HEADS), f32, math.log(1e-3), math.log(1e-1)))
    gdn_dt_bias = dt + jnp.log(-jnp.expm1(-dt))
    gdn_norm_gain = 1.0 + 0.05 * jax.random.normal(ks[9], (DEPTH, HEAD_DIM), f32)
    w_o = jax.random.normal(ks[10], (DEPTH, MIX_DIM, D_MODEL), f32) * MIX_DIM ** -0.5
    norm_ffn_gain = 1.0 + 0.05 * jax.random.normal(ks[11], (DEPTH, D_MODEL), f32)
    w_up = jax.random.normal(ks[12], (DEPTH, D_MODEL, 2 * FFN_DIM), f32) * D_MODEL ** -0.5
    ffn_conv_w = jax.random.normal(ks[13], (DEPTH, FFN_CONV, 2 * FFN_DIM), f32) * FFN_CONV ** -0.5
    w_down = jax.random.normal(ks[14], (DEPTH, FFN_DIM, D_MODEL), f32) * FFN_DIM ** -0.5
    final_norm_gain = 1.0 + 0.05 * jax.random.normal(ks[15], (D_MODEL,), f32)
    return {"x": x, "norm_mix_gain": norm_mix_gain, "w_in": w_in, "pool_w": pool_w,
            "pool_scale": pool_scale, "fox_f_bias": fox_f_bias, "gdn_conv_w": gdn_conv_w,
            "gdn_A_log": gdn_A_log, "gdn_dt_bias": gdn_dt_bias, "gdn_norm_gain": gdn_norm_gain,
            "w_o": w_o, "norm_ffn_gain": norm_ffn_gain, "w_up": w_up, "ffn_conv_w": ffn_conv_w,
            "w_down": w_down, "final_norm_gain": final_norm_gain}


def reference(x, norm_mix_gain, w_in, pool_w, pool_scale, fox_f_bias, gdn_conv_w, gdn_A_log,
              gdn_dt_bias, gdn_norm_gain, w_o, norm_ffn_gain, w_up, ffn_conv_w, w_down,
              final_norm_gain):
    b, t, _ = x.shape
    f32 = jnp.float32
    for l in range(DEPTH):
        h = rms_norm(x, norm_mix_gain[l])
        proj = h @ w_in[l]
        pool_in, fox_qkv, fox_f, gdn_qkv, gdn_z, gdn_b, gdn_a = split_columns(proj)

        y_pool = multiscale_pool_mixer(pool_in, pool_w[l], pool_scale[l])

        fq, fk, fv = (a.reshape(b, t, FOX_HEADS, HEAD_DIM) for a in jnp.split(fox_qkv, 3, axis=-1))
        log_f = jax.nn.log_sigmoid(fox_f.astype(f32) + fox_f_bias[l].astype(f32))
        y_fox = forgetting_attention(fq, fk, fv, log_f).reshape(b, t, ATTN_DIM)

        gqkv = jax.nn.silu(causal_depthwise_conv(gdn_qkv, gdn_conv_w[l]))
        gq, gk, gv = (a.reshape(b, t, GDN_HEADS, HEAD_DIM).astype(f32) for a in jnp.split(gqkv, 3, axis=-1))
        beta = jax.nn.sigmoid(gdn_b.astype(f32))
        g = -jnp.exp(gdn_A_log[l].astype(f32)) * jax.nn.softplus(gdn_a.astype(f32) + gdn_dt_bias[l].astype(f32))
        o = gated_delta_rule(gq, gk, gv, g, beta)
        z = gdn_z.reshape(b, t, GDN_HEADS, HEAD_DIM).astype(f32)
        o = rms_norm(o, gdn_norm_gain[l]) * jax.nn.silu(z)
        y_gdn = o.reshape(b, t, GDN_DIM).astype(x.dtype)

        mix = jnp.concatenate([y_pool.astype(x.dtype), y_fox.astype(x.dtype), y_gdn], axis=-1)
        x = x + mix @ w_o[l]

        h = rms_norm(x, norm_ffn_gain[l])
        u = causal_depthwise_conv(h @ w_up[l], ffn_conv_w[l])
        gate, up = jnp.split(u, 2, axis=-1)
        x = x + (jax.nn.silu(gate) * up) @ w_down[l]
    return rms_norm(x, final_norm_gain)
```

```python
import math
from contextlib import ExitStack
import numpy as np
import ml_dtypes
import concourse.bass as bass
import concourse.mybir as mybir
from concourse.bass_utils import run_bass_kernel_spmd

F32 = mybir.dt.float32
BF16 = mybir.dt.bfloat16
AF = mybir.ActivationFunctionType
ALU = mybir.AluOpType
AX = mybir.AxisListType
NPBF = ml_dtypes.bfloat16


class Cfg:
    def __init__(self, D=4096, T=4096, FF=11008, NB=4):
        self.D, self.T, self.FF, self.NB = D, T, FF, NB
        self.KC = D // 128
        self.GD = D // 16
        self.POOL = 4 * self.GD
        self.ATT = (D - self.POOL) // 2
        self.H = self.ATT // 128
        self.HC = self.H // 2
        self.TH = T // 2
        self.FC = FF // 128
        self.PC = 2 * self.GD
        self.AC = self.HC * 128
        self.NCOL = self.PC + 7 * self.AC + 128
        self.NMC = self.NCOL // 128


class Op:
    __slots__ = ("eng", "fn", "reads", "writes", "dma_key", "waits", "signal", "sigval", "idx")


class Prog:
    ENGS = ("pe", "act", "dve", "pool", "sp")

    def __init__(self, nc, gstate):
        self.nc = nc
        self.g = gstate
        self.ops = []
        self.stack = ExitStack()
        self._psum_rr = 0
        self.psum = []
        self.pid = gstate["pid"]
        gstate["pid"] += 1

    def sbuf(self, name, shape, dtype):
        t = self.stack.enter_context(self.nc.sbuf_tensor(f"p{self.pid}_{name}", list(shape), dtype))
        return t

    def pool_(self, name, shape, dtype, n):
        return [(self.sbuf(f"{name}{i}", shape, dtype), f"{name}{i}") for i in range(n)]

    def alloc_psum(self, n=8):
        for i in range(n):
            t = self.stack.enter_context(self.nc.psum_tensor(f"p{self.pid}_ps{i}", [128, 512], F32))
            self.psum.append((t, f"ps{i}"))

    def ps(self):
        r = self.psum[self._psum_rr % len(self.psum)]
        self._psum_rr += 1
        return r

    def op(self, eng, fn, reads=(), writes=(), dma_key=None):
        o = Op()
        o.eng, o.fn, o.reads, o.writes, o.dma_key = eng, fn, tuple(reads), tuple(writes), dma_key
        o.waits, o.signal, o.sigval, o.idx = [], False, None, len(self.ops)
        self.ops.append(o)
        return o

    def dma(self, eng, out, in_, key, reads=(), writes=()):
        return self.op(eng, lambda e: e.dma_start(out=out, in_=in_), reads, writes, dma_key=key)

    def run(self):
        nc, g = self.nc, self.g
        last_w, readers = {}, {}
        deps = []
        for o in self.ops:
            d = set()
            for r in o.reads:
                d.update(last_w.get(r, {}).values())
            for w in o.writes:
                d.update(last_w.get(w, {}).values())
                d.update(readers.get(w, ()))
            d.discard(o.idx)
            deps.append(d)
            sk = o.eng if o.dma_key is None else ("d", o.dma_key)
            for r in o.reads:
                readers.setdefault(r, []).append(o.idx)
            for w in o.writes:
                last_w.setdefault(w, {})[sk] = o.idx
                readers[w] = []
        for o, d in zip(self.ops, deps):
            for j in d:
                pj = self.ops[j]
                if pj.dma_key is None:
                    if pj.eng == "pe" and o.eng == "pe" and o.dma_key is None:
                        continue
                    pj.signal = True
        last_of = {}
        for o in self.ops:
            if o.dma_key is None:
                last_of[o.eng] = o
        for o in last_of.values():
            o.signal = True
        for o in self.ops:
            if o.dma_key is not None:
                g["dma_cnt"][o.dma_key] = g["dma_cnt"].get(o.dma_key, 0) + 16
                o.sigval = g["dma_cnt"][o.dma_key]
            elif o.signal:
                g["eng_cnt"][o.eng] += 1
                o.sigval = g["eng_cnt"][o.eng]
        for o, d in zip(self.ops, deps):
            need = {}
            for j in d:
                pj = self.ops[j]
                if pj.dma_key is not None:
                    k = ("d", pj.dma_key)
                else:
                    if pj.eng == "pe" and o.eng == "pe" and o.dma_key is None:
                        continue
                    k = ("e", pj.eng)
                need[k] = max(need.get(k, 0), pj.sigval)
            o.waits = need
        by_eng = {e: [o for o in self.ops if o.eng == e] for e in self.ENGS}
        final_eng = dict(g["eng_cnt"])
        final_dma = dict(g["dma_cnt"])
        for k in final_dma:
            self._dsem(k)

        def emit(eng_name, e):
            waited = g["waited"][eng_name]
            for o in by_eng[eng_name]:
                for k, v in o.waits.items():
                    if waited.get(k, 0) >= v:
                        continue
                    waited[k] = v
                    sem = g["esem"][k[1]] if k[0] == "e" else self._dsem(k[1])
                    e.wait_ge(sem, v)
                ins = o.fn(e)
                if o.dma_key is not None:
                    ins.then_inc(self._dsem(o.dma_key), 16)
                elif o.signal:
                    ins.then_inc(g["esem"][eng_name], 1)
            for en, v in final_eng.items():
                k = ("e", en)
                if v > 0 and waited.get(k, 0) < v:
                    waited[k] = v
                    e.wait_ge(g["esem"][en], v)
            for dk, v in final_dma.items():
                k = ("d", dk)
                if waited.get(k, 0) < v:
                    waited[k] = v
                    e.wait_ge(self._dsem(dk), v)

        with nc.Block() as block:
            @block.sync
            def _(e):
                emit("sp", e)

            @block.tensor
            def _(e):
                emit("pe", e)

            @block.scalar
            def _(e):
                emit("act", e)

            @block.vector
            def _(e):
                emit("dve", e)

            @block.gpsimd
            def _(e):
                emit("pool", e)
        self.stack.close()

    def _dsem(self, key):
        g = self.g
        if key not in g["dsem"]:
            g["dsem"][key] = g["stack"].enter_context(self.nc.semaphore(f"d_{key}"))
        return g["dsem"][key]


def new_gstate(nc, stack):
    g = {"stack": stack, "pid": 0, "dsem": {}, "dma_cnt": {}, "eng_cnt": {e: 0 for e in Prog.ENGS},
         "esem": {}, "waited": {e: {} for e in Prog.ENGS}}
    for e in Prog.ENGS:
        g["esem"][e] = stack.enter_context(nc.semaphore(f"e_{e}"))
    return g


def col_tiles(n0, ntot, w=512):
    out = []
    c = n0
    while c < ntot:
        out.append((c, min(w, ntot - c)))
        c += w
    return out


def gemm_fm(P, wt_dram, mcs, KC, act, act_key, tiles, evac, wpool, wq="pool"):
    for i, mc in enumerate(mcs):
        wt, wk = wpool[i % len(wpool)]
        P.dma(wq, wt[:, :, :], wt_dram[mc], wk, writes=(wk,))
        pst = []
        for (c0, n) in tiles:
            ps, pk = P.ps()
            pst.append((ps, pk, c0, n))
        for (ps, pk, c0, n) in pst:
            for kc in range(KC):
                P.op("pe", (lambda e, ps=ps, wt=wt, kc=kc, c0=c0, n=n:
                            e.matmul(ps[:, 0:n], wt[:, kc, :], act[:, kc, c0:c0 + n],
                                     start=(kc == 0), stop=(kc == KC - 1))),
                     reads=(wk, act_key), writes=(pk,))
        evac(mc, pst)


def rmsnorm_fm(P, cfg, x_dram, h_dram, gain_sb, gain_key, ncols, ones_bf, eps=1e-6, wbk=256, out_f32=False):
    KC, D = cfg.KC, cfg.D
    xv = x_dram.rearrange("(kc p) n -> p kc n", p=128)
    hv = h_dram.rearrange("(kc p) n -> p kc n", p=128)
    xp = P.pool_("nx", [128, KC, wbk], F32, 2)
    sp_ = P.pool_("nsq", [128, KC, wbk], BF16, 1)
    hp = P.pool_("nh", [128, KC, wbk], F32 if out_f32 else BF16, 2)
    rp = P.pool_("nr", [128, wbk], F32, 2)
    for i, (c0, n) in enumerate(col_tiles(0, ncols, wbk)):
        xt, xk = xp[i % 2]
        sq, sk = sp_[0]
        ht, hk = hp[i % 2]
        rt, rk = rp[i % 2]
        P.dma("sp", xt[:, :, 0:n], xv[:, :, c0:c0 + n], xk, writes=(xk,))
        P.op("act", lambda e, sq=sq, xt=xt, n=n: e.activation(out=sq[:, :, 0:n], in_=xt[:, :, 0:n], func=AF.Square),
             reads=(xk,), writes=(sk,))
        ps, pk = P.ps()
        for kc in range(KC):
            P.op("pe", lambda e, ps=ps, sq=sq, kc=kc, n=n: e.matmul(ps[:, 0:n], ones_bf[:, :], sq[:, kc, 0:n],
                                                                     start=(kc == 0), stop=(kc == KC - 1)),
                 reads=(sk,), writes=(pk,))
        P.op("act", lambda e, rt=rt, ps=ps, n=n: e.activation(out=rt[:, 0:n], in_=ps[:, 0:n], func=AF.Ln,
                                                              scale=1.0 / D, bias=P.eps_ap),
             reads=(pk,), writes=(rk,))
        P.op("act", lambda e, rt=rt, n=n: e.activation(out=rt[:, 0:n], in_=rt[:, 0:n], func=AF.Exp, scale=-0.5),
             reads=(rk,), writes=(rk,))
        for kc in range(KC):
            eng = "dve"
            P.op(eng, lambda e, ht=ht, xt=xt, rt=rt, kc=kc, n=n: e.scalar_tensor_tensor(
                out=ht[:, kc, 0:n], in0=xt[:, kc, 0:n], scalar=gain_sb[:, kc:kc + 1], in1=rt[:, 0:n],
                op0=ALU.mult, op1=ALU.mult), reads=(xk, rk, gain_key), writes=(hk,))
        P.dma("sp", hv[:, :, c0:c0 + n], ht[:, :, 0:n], hk, reads=(hk,))


def consts(P, cfg):
    ones_bf = P.sbuf("ones_bf", [128, 128], BF16)
    P.op("pool", lambda e: e.memset(ones_bf[:, :], 1.0), writes=("ones_bf",))
    eps = P.sbuf("eps_t", [128, 16], F32)
    P.op("pool", lambda e: e.memset(eps[:, :], 1e-6), writes=("eps_t",))
    P.eps_ap = eps[:, 0:1]
    return ones_bf


def build_ffn_launch(cfg, final):
    D, KC, TH, FF, FC = cfg.D, cfg.KC, cfg.TH, cfg.FF, cfg.FC
    NT = TH + 2
    nc = bass.Bass("TRN2", target_bir_lowering=False)
    dt = nc.dram_tensor
    xT = dt("xT", [D, NT], F32, kind="ExternalInput").ap()
    mixT = dt("mixT", [D, NT], BF16, kind="ExternalInput").ap()
    wo = dt("wo", [KC, 128, KC, 128], F32, kind="ExternalInput").ap()
    wup = dt("wup", [2 * FC, 128, KC, 128], F32, kind="ExternalInput").ap()
    wdn = dt("wdn", [KC, 128, FC, 128], F32, kind="ExternalInput").ap()
    gains = dt("gains", [128, 2, KC], F32, kind="ExternalInput").ap()
    convw = dt("convw", [128, 2 * FC, 3], F32, kind="ExternalInput").ap()
    out = dt("out", [D, TH], F32, kind="ExternalOutput").ap()
    x1T = dt("x1T", [D, NT], F32, kind="Internal").ap()
    hT = dt("hT", [D, NT], BF16, kind="Internal").ap()
    actT = dt("actT", [FF, TH], BF16, kind="Internal").ap()
    x2T = dt("x2T", [D, TH], F32, kind="Internal").ap() if final else out
    with ExitStack() as gs:
        g = new_gstate(nc, gs)
        for p in range(2):
            P = Prog(nc, g)
            P.alloc_psum()
            pb = 0 if p == 0 else 2 + TH // 2
            NP = (2 + TH // 2) if p == 0 else TH // 2
            act = P.sbuf("mix_res", [128, KC, NP], BF16)
            P.dma("sp", act[:, :, :], mixT.rearrange("(kc p) n -> p kc n", p=128)[:, :, pb:pb + NP],
                  "mix_res", writes=("mix_res",))
            wpool = P.pool_("w", [128, KC, 128], BF16, 3)
            xin = P.pool_("xin", [128, 512], F32, 4)
            tiles = col_tiles(0, NP)
            cnt = [0]

            def evac1(mc, pst, pb=pb, P=P, xin=xin, cnt=cnt):
                for (ps, pk, c0, n) in pst:
                    i = cnt[0]; cnt[0] += 1
                    xi, xik = xin[i % 4]
                    g0 = pb + c0
                    P.dma("sp", xi[:, 0:n], xT[mc * 128:(mc + 1) * 128, g0:g0 + n], xik, writes=(xik,))
                    P.op("dve", lambda e, ps=ps, n=n, xi=xi: e.tensor_tensor(
                        out=xi[:, 0:n], in0=ps[:, 0:n], in1=xi[:, 0:n], op=ALU.add),
                        reads=(pk, xik), writes=(xik,))
                    P.dma("sp", x1T[mc * 128:(mc + 1) * 128, g0:g0 + n], xi[:, 0:n], xik, reads=(xik,))
            gemm_fm(P, wo, list(range(KC)), KC, act, "mix_res", tiles, evac1, wpool)
            P.run()
        P = Prog(nc, g)
        P.alloc_psum()
        ones_bf = consts(P, cfg)
        gsb = P.sbuf("gains_sb", [128, 2, KC], F32)
        P.dma("sp", gsb[:, :, :], gains, "gains_sb", writes=("gains_sb",))
        rmsnorm_fm(P, cfg, x1T, hT, gsb[:, 0, :], "gains_sb", NT, ones_bf)
        P.run()
        THp = TH // 2
        NTp = THp + 2
        for p in range(2):
            P = Prog(nc, g)
            P.alloc_psum()
            hres = P.sbuf("h_res", [128, KC, NTp], BF16)
            P.dma("sp", hres[:, :, :], hT.rearrange("(kc p) n -> p kc n", p=128)[:, :, p * THp:p * THp + NTp],
                  "h_res", writes=("h_res",))
            cw = P.sbuf("convw_sb", [128, 2 * FC, 3], F32)
            P.dma("sp", cw[:, :, :], convw, "convw_sb", writes=("convw_sb",))
            wpool = P.pool_("w", [128, KC, 128], BF16, 3)
            up_ = P.pool_("u", [128, NTp], F32, 4)
            cp_ = P.pool_("c", [128, THp], F32, 4)
            ap_ = P.pool_("a", [128, THp], BF16, 2)
            st = {"n": 0, "gate": None}
            tiles = col_tiles(0, NTp)

            def evac3(mc, pst, p=p, st=st, up_=up_, cp_=cp_, ap_=ap_, cw=cw, P=P):
                i = st["n"]; st["n"] += 1
                ut, uk = up_[i % 4]
                ct, ck = cp_[i % 4]
                for (ps, pk, c0, n) in pst:
                    P.op("act", lambda e, ps=ps, c0=c0, n=n, ut=ut: e.activation(out=ut[:, c0:c0 + n], in_=ps[:, 0:n], func=AF.Copy),
                         reads=(pk,), writes=(uk,))
                eng = "dve"
                P.op(eng, lambda e: e.tensor_scalar(out=ct[:, :], in0=ut[:, 0:THp], scalar1=cw[:, mc, 0:1], scalar2=None, op0=ALU.mult),
                     reads=(uk, "convw_sb"), writes=(ck,))
                P.op(eng, lambda e: e.scalar_tensor_tensor(out=ct[:, :], in0=ut[:, 1:THp + 1], scalar=cw[:, mc, 1:2], in1=ct[:, :],
                                                           op0=ALU.mult, op1=ALU.add), reads=(uk, ck, "convw_sb"), writes=(ck,))
                P.op(eng, lambda e: e.scalar_tensor_tensor(out=ct[:, :], in0=ut[:, 2:THp + 2], scalar=cw[:, mc, 2:3], in1=ct[:, :],
                                                           op0=ALU.mult, op1=ALU.add), reads=(uk, ck, "convw_sb"), writes=(ck,))
                if mc < FC:
                    P.op("act", lambda e: e.activation(out=ct[:, :], in_=ct[:, :], func=AF.Silu), reads=(ck,), writes=(ck,))
                    st["gate"] = (ct, ck)
                else:
                    gt, gk = st["gate"]
                    at, ak = ap_[(i // 2) % 2]
                    P.op("dve", lambda e: e.tensor_tensor(out=at[:, :], in0=gt[:, :], in1=ct[:, :], op=ALU.mult),
                         reads=(gk, ck), writes=(ak,))
                    j = mc - FC
                    P.dma("sp", actT[j * 128:(j + 1) * 128, p * THp:(p + 1) * THp], at[:, :], ak, reads=(ak,))
            order = []
            for j in range(FC):
                order += [j, FC + j]
            gemm_fm(P, wup, order, KC, hres, "h_res", tiles, evac3, wpool)
            P.run()
        P = Prog(nc, g)
        P.alloc_psum()
        TB = 512
        ares = P.pool_("a_res", [128, FC, TB], BF16, 1)
        wpool = P.pool_("wd", [128, FC, 128], BF16, 2)
        xin = P.pool_("xin", [128, TB], F32, 3)
        actv = actT.rearrange("(fc p) n -> p fc n", p=128)
        for tb in range(TH // TB):
            at, ak = ares[0]
            P.dma("sp", at[:, :, :], actv[:, :, tb * TB:(tb + 1) * TB], ak, writes=(ak,))
            cnt = [0]

            def evac4(mc, pst, tb=tb):
                i = cnt[0]; cnt[0] += 1
                xi, xik = xin[i % 3]
                P.dma("sp", xi[:, :], x1T[mc * 128:(mc + 1) * 128, 2 + tb * TB:2 + (tb + 1) * TB], xik, writes=(xik,))
                (ps, pk, c0, n) = pst[0]
                P.op("dve", lambda e: e.tensor_tensor(out=xi[:, :], in0=ps[:, 0:TB], in1=xi[:, :], op=ALU.add),
                     reads=(pk, xik), writes=(xik,))
                P.dma("sp", x2T[mc * 128:(mc + 1) * 128, tb * TB:(tb + 1) * TB], xi[:, :], xik, reads=(xik,))
            gemm_fm(P, wdn, list(range(KC)), FC, at, ak, [(0, TB)], evac4, wpool)
        P.run()
        if final:
            P = Prog(nc, g)
            P.alloc_psum()
            ones_bf = consts(P, cfg)
            gsb = P.sbuf("gains_sb", [128, 2, KC], F32)
            P.dma("sp", gsb[:, :, :], gains, "gains_sb", writes=("gains_sb",))
            rmsnorm_fm(P, cfg, x2T, out, gsb[:, 1, :], "gains_sb", TH, ones_bf, out_f32=True)
            P.run()
    return nc


def sub_ps(P, names):
    lst = [x for x in P.psum if x[1] in names]
    st = {"i": 0}

    def nxt():
        r = lst[st["i"] % len(lst)]
        st["i"] += 1
        return r
    return nxt


def build_mixer_launch(cfg, stop=9):
    D, KC, T, HC, GD, PC, AC, NCOL, NMC = cfg.D, cfg.KC, cfg.T, cfg.HC, cfg.GD, cfg.PC, cfg.AC, cfg.NCOL, cfg.NMC
    GC = GD // 128
    NCH = T // 128
    SQD = math.sqrt(128.0)
    nc = bass.Bass("TRN2", target_bir_lowering=False)
    dt = nc.dram_tensor
    xT = dt("xT", [D, T], F32, kind="ExternalInput").ap()
    gain = dt("gain", [128, KC], F32, kind="ExternalInput").ap()
    win = dt("win", [NMC, 128, KC, 128], F32, kind="ExternalInput").ap()
    poolw = dt("poolw", [2 * GC, 128, GC, 128], F32, kind="ExternalInput").ap()
    poolsc = dt("poolsc", [128, 2 * GC], F32, kind="ExternalInput").ap()
    poolcoef = dt("poolcoef", [2, 128, 4, T], F32, kind="ExternalInput").ap()
    foxb = dt("foxb", [HC, 1], F32, kind="ExternalInput").ap()
    gconv = dt("gconv", [128, 3 * HC, 4], F32, kind="ExternalInput").ap()
    galog = dt("galog", [HC, 1], F32, kind="ExternalInput").ap()
    gdtb = dt("gdtb", [HC, 1], F32, kind="ExternalInput").ap()
    gnrep = dt("gnrep", [128, 128], F32, kind="ExternalInput").ap()
    cmask = dt("cmask", [5, 128, 128], F32, kind="ExternalInput").ap()
    cmaskb = dt("cmaskb", [2, 128, 128], BF16, kind="ExternalInput").ap()
    mixo = dt("mixo", [PC + 2 * AC, T], BF16, kind="ExternalOutput").ap()
    hT = dt("hT", [D, T], BF16, kind="Internal").ap()
    projT = dt("projT", [NCOL, T], F32, kind="Internal").ap()
    augd = dt("augd", [6, HC, T], BF16, kind="Internal").ap()
    qkd = dt("qkd", [128, 2 * HC, T], BF16, kind="Internal").ap()
    kvd = dt("kvd", [128, 2 * HC, T], F32, kind="Internal").ap()
    c_fq, c_fk, c_fv = PC, PC + AC, PC + 2 * AC
    c_gq, c_gk, c_gv, c_gz = PC + 3 * AC, PC + 4 * AC, PC + 5 * AC, PC + 6 * AC
    c_sc = PC + 7 * AC
    with ExitStack() as gs:
        g = new_gstate(nc, gs)
        P = Prog(nc, g)
        P.alloc_psum()
        ones_bf = consts(P, cfg)
        gsb = P.sbuf("gain_sb", [128, KC], F32)
        P.dma("sp", gsb[:, :], gain, "gain_sb", writes=("gain_sb",))
        rmsnorm_fm(P, cfg, xT, hT, gsb, "gain_sb", T, ones_bf)
        P.run()
        if stop < 2:
            return nc
        NPASS = 2 if T >= 2048 else 1
        TP = T // NPASS
        for p in range(NPASS):
            P = Prog(nc, g)
            P.alloc_psum()
            hres = P.sbuf("h_res", [128, KC, TP], BF16)
            P.dma("sp", hres[:, :, :], hT.rearrange("(kc p) n -> p kc n", p=128)[:, :, p * TP:(p + 1) * TP],
                  "h_res", writes=("h_res",))
            wpool = P.pool_("w", [128, KC, 128], BF16, 3)
            ev = P.pool_("ev", [128, 512], F32, 4)
            cnt = [0]

            def evac2(mc, pst, p=p, P=P, ev=ev, cnt=cnt):
                for (ps, pk, c0, n) in pst:
                    i = cnt[0]; cnt[0] += 1
                    et, ek = ev[i % 4]
                    if i % 2 == 0:
                        P.op("act", lambda e, ps=ps, et=et, n=n: e.activation(out=et[:, 0:n], in_=ps[:, 0:n], func=AF.Copy),
                             reads=(pk,), writes=(ek,))
                    else:
                        P.op("dve", lambda e, ps=ps, et=et, n=n: e.tensor_copy(out=et[:, 0:n], in_=ps[:, 0:n]),
                             reads=(pk,), writes=(ek,))
                    P.dma("sp", projT[mc * 128:(mc + 1) * 128, p * TP + c0:p * TP + c0 + n], et[:, 0:n], ek, reads=(ek,))
            gemm_fm(P, win, list(range(NMC)), KC, hres, "h_res", col_tiles(0, TP), evac2, wpool)
            P.run()
        if stop < 3:
            return nc
        P = Prog(nc, g)
        P.alloc_psum()
        pw = P.sbuf("pw", [128, 2 * GC, GC, 128], BF16)
        for ge in range(2 * GC):
            P.dma("pool", pw[:, ge, :, :], poolw[ge], "pw", writes=("pw",))
        psc = P.sbuf("psc", [128, 2 * GC], F32)
        P.dma("sp", psc[:, :], poolsc, "psc", writes=("psc",))
        coef = P.sbuf("coef", [128, 4, T], F32)
        vpool = P.pool_("pv", [128, 16 + T], F32, 1)
        spool = P.pool_("psm", [128, 16 + T], F32, 2)
        acc = P.pool_("pacc", [128, T], F32, 1)
        tmp = P.pool_("ptmp", [128, T], F32, 1)
        pooled = P.pool_("pooled", [128, GC, T], BF16, 1)
        yo = P.pool_("pyo", [128, 512], BF16, 3)
        for (t_, k_) in vpool + spool:
            P.op("pool", lambda e, t_=t_: e.memset(t_[:, 0:16], 0.0), writes=(k_,))
        for gi in range(2):
            P.dma("sp", coef[:, :, :], poolcoef[gi], "coef", writes=("coef",))
            pl, plk = pooled[0]
            for cc in range(GC):
                vt, vk = vpool[0]
                r0 = gi * GD + cc * 128
                P.dma("sp", vt[:, 16:16 + T], projT[r0:r0 + 128, :], vk, writes=(vk,))
                at, ak = acc[0]
                tt, tk = tmp[0]
                src, srck = vt, vk
                for wi, sh in enumerate((1, 2, 4, 8)):
                    st_, sk_ = spool[wi % 2]
                    P.op("dve", lambda e, st_=st_, src=src, sh=sh: e.tensor_tensor(
                        out=st_[:, 16:16 + T], in0=src[:, 16:16 + T], in1=src[:, 16 - sh:16 + T - sh], op=ALU.add),
                        reads=(srck,), writes=(sk_,))
                    if wi == 0:
                        P.op("pool", lambda e, st_=st_, wi=wi, at=at: e.tensor_tensor(
                            out=at[:, :], in0=st_[:, 16:16 + T], in1=coef[:, wi, :], op=ALU.mult),
                            reads=(sk_, "coef"), writes=(ak,))
                    else:
                        P.op("pool", lambda e, st_=st_, wi=wi, tt=tt: e.tensor_tensor(
                            out=tt[:, :], in0=st_[:, 16:16 + T], in1=coef[:, wi, :], op=ALU.mult),
                            reads=(sk_, "coef"), writes=(tk,))
                        P.op("dve", lambda e, at=at, tt=tt: e.tensor_tensor(out=at[:, :], in0=at[:, :], in1=tt[:, :], op=ALU.add),
                             reads=(ak, tk), writes=(ak,))
                    src, srck = st_, sk_
                P.op("dve", lambda e, at=at, vt=vt, pl=pl, cc=cc: e.tensor_tensor(
                    out=pl[:, cc, :], in0=at[:, :], in1=vt[:, 16:16 + T], op=ALU.subtract),
                    reads=(ak, vk), writes=(plk,))
            for ec in range(GC):
                for ti, (c0, n) in enumerate(col_tiles(0, T)):
                    ps, pk = P.ps()
                    for cc in range(GC):
                        P.op("pe", lambda e, ps=ps, gi=gi, ec=ec, cc=cc, c0=c0, n=n, pl=pl: e.matmul(
                            ps[:, 0:n], pw[:, gi * GC + ec, cc, :], pl[:, cc, c0:c0 + n], start=(cc == 0), stop=(cc == GC - 1)),
                            reads=("pw", plk), writes=(pk,))
                    yt, yk = yo[(ec * 8 + ti) % 3]
                    P.op("act", lambda e, ps=ps, yt=yt, n=n, gi=gi, ec=ec: e.activation(
                        out=yt[:, 0:n], in_=ps[:, 0:n], func=AF.Copy, scale=psc[:, gi * GC + ec:gi * GC + ec + 1]),
                        reads=(pk, "psc"), writes=(yk,))
                    r0 = gi * GD + ec * 128
                    P.dma("sp", mixo[r0:r0 + 128, c0:c0 + n], yt[:, 0:n], yk, reads=(yk,))
        P.run()
        if stop < 4:
            return nc
        build_fox(nc, g, cfg, projT, augd, foxb, cmaskb, cmask, mixo, c_fq, c_fk, c_fv, c_sc)
        if stop < 5:
            return nc
        build_gdn(nc, g, cfg, projT, qkd, kvd, gconv, galog, gdtb, gnrep, cmask, mixo, c_gq, c_gk, c_gv, c_gz, c_sc, stop=stop)
    return nc


def build_fox(nc, g, cfg, projT, augd, foxb, cmaskb, cmask, mixo, c_fq, c_fk, c_fv, c_sc):
    T, HC, PC = cfg.T, cfg.HC, cfg.PC
    SQD = math.sqrt(128.0)
    P = Prog(nc, g)
    f = P.sbuf("f", [HC, T], F32)
    z0 = P.sbuf("z0", [HC, T], F32)
    r = P.sbuf("r", [HC, T], F32)
    fb = P.sbuf("fb", [HC, 1], F32)
    parts = [P.sbuf(f"pt{i}", [HC, T], BF16) for i in range(6)]
    P.dma("sp", f[:, :], projT[c_sc:c_sc + HC, :], "f", writes=("f",))
    P.dma("sp", fb[:, :], foxb, "fb", writes=("fb",))
    P.op("pool", lambda e: e.memset(z0[:, :], 0.0), writes=("z0",))
    P.op("dve", lambda e: e.tensor_scalar(out=f[:, :], in0=f[:, :], scalar1=fb[:, 0:1], scalar2=None, op0=ALU.add),
         reads=("f", "fb"), writes=("f",))
    P.op("act", lambda e: e.activation(out=f[:, :], in_=f[:, :], func=AF.Exp, scale=-1.0), reads=("f",), writes=("f",))
    P.op("act", lambda e: e.activation(out=f[:, :], in_=f[:, :], func=AF.Ln, bias=1.0), reads=("f",), writes=("f",))
    P.op("dve", lambda e: e.tensor_tensor_scan(out=r[:, :], data0=f[:, :], data1=z0[:, :], initial=0.0, op0=ALU.add, op1=ALU.add),
         reads=("f", "z0"), writes=("r",))
    P.op("dve", lambda e: e.tensor_scalar(out=r[:, :], in0=r[:, :], scalar1=SQD, scalar2=None, op0=ALU.mult),
         reads=("r",), writes=("r",))
    for i in range(3):
        P.op("dve", lambda e, i=i: e.tensor_copy(out=parts[i][:, :], in_=r[:, :]), reads=("r",), writes=(f"pt{i}",))
        P.op("dve", lambda e, i=i: e.tensor_scalar(out=parts[3 + i][:, :], in0=parts[i][:, :], scalar1=-1.0, scalar2=None, op0=ALU.mult),
             reads=(f"pt{i}",), writes=(f"pt{3 + i}",))
        if i < 2:
            P.op("dve", lambda e, i=i: e.tensor_tensor(out=r[:, :], in0=r[:, :], in1=parts[i][:, :], op=ALU.subtract),
                 reads=("r", f"pt{i}"), writes=("r",))
    for i in range(6):
        P.dma("sp", augd[i], parts[i][:, :], f"pt{i}", reads=(f"pt{i}",))
    P.run()
    P = Prog(nc, g)
    P.alloc_psum()
    ps_s = sub_ps(P, ("ps0", "ps1", "ps2", "ps3"))
    ps_acc = sub_ps(P, ("ps4", "ps5", "ps6", "ps7"))
    ones_bf = consts(P, cfg)
    idf = P.sbuf("idf", [128, 128], F32)
    P.dma("sp", idf[:, :], cmask[0], "idf", writes=("idf",))
    idb = P.sbuf("idb", [128, 128], BF16)
    mneg = P.sbuf("mneg", [128, 128], BF16)
    P.dma("sp", idb[:, :], cmaskb[0], "idb", writes=("idb",))
    P.dma("sp", mneg[:, :], cmaskb[1], "mneg", writes=("mneg",))
    qp = P.pool_("fq", [128, T], BF16, 2)
    kp = P.pool_("fk", [128, T], BF16, 2)
    vp = P.pool_("fv", [128, T], F32, 1)
    vtp = P.pool_("fvt", [128, T // 128, 128], BF16, 2)
    kap = P.pool_("kaug", [6, T], BF16, 2)
    qap = P.pool_("qaug", [6, T], BF16, 2)
    ptp = P.pool_("pt", [128, 512], BF16, 4)
    rcp = P.pool_("rc", [128, 512], F32, 2)
    yop = P.pool_("fy", [128, 512], BF16, 2)
    for h in range(HC):
        qt, qk = qp[h % 2]; kt, kk = kp[h % 2]; vt, vk = vp[0]; vtok, vtk = vtp[h % 2]
        ka, kak = kap[h % 2]; qa, qak = qap[h % 2]
        P.dma("pool", qt[:, :], projT[c_fq + h * 128:c_fq + (h + 1) * 128, :], qk, writes=(qk,))
        P.dma("pool", kt[:, :], projT[c_fk + h * 128:c_fk + (h + 1) * 128, :], kk, writes=(kk,))
        P.dma("sp", vt[:, :], projT[c_fv + h * 128:c_fv + (h + 1) * 128, :], vk, writes=(vk,))
        P.op("dve", lambda e, ka=ka: e.memset(ka[:, :], 1.0), writes=(kak,))
        P.op("dve", lambda e, qa=qa: e.memset(qa[:, :], 1.0), writes=(qak,))
        for i in range(3):
            P.dma("sp", ka[i:i + 1, :], augd[i, h:h + 1, :], kak, writes=(kak,))
            P.dma("sp", qa[3 + i:4 + i, :], augd[3 + i, h:h + 1, :], qak, writes=(qak,))
        for b in range(T // 128):
            ps, pk = ps_s()
            P.op("pe", lambda e, ps=ps, vt=vt, b=b: e.transpose(ps[:, 0:128], vt[:, b * 128:(b + 1) * 128], idf[:, :]),
                 reads=(vk, "idf"), writes=(pk,))
            P.op("act" if b % 2 else "dve", (lambda e, ps=ps, vtok=vtok, b=b: e.activation(out=vtok[:, b, :], in_=ps[:, 0:128], func=AF.Copy))
                 if b % 2 else (lambda e, ps=ps, vtok=vtok, b=b: e.tensor_copy(out=vtok[:, b, :], in_=ps[:, 0:128])),
                 reads=(pk,), writes=(vtk,))
        for qb in range(T // 512):
            t0 = qb * 512
            ops_, opk = ps_acc()
            lps, lpk = ps_acc()
            nkb = (t0 + 512) // 128
            for kb in range(nkb):
                s0 = kb * 128
                off = max(0, s0 - t0)
                n = 512 - off
                diag = s0 >= t0
                ps, pk = ps_s()
                P.op("pe", lambda e, ps=ps, kt=kt, qt=qt, s0=s0, t0=t0, off=off, n=n: e.matmul(
                    ps[:, 0:n], kt[:, s0:s0 + 128], qt[:, t0 + off:t0 + 512], start=True, stop=False),
                    reads=(kk, qk), writes=(pk,))
                P.op("pe", lambda e, ps=ps, ka=ka, qa=qa, s0=s0, t0=t0, off=off, n=n, diag=diag: e.matmul(
                    ps[:, 0:n], ka[:, s0:s0 + 128], qa[:, t0 + off:t0 + 512], start=False, stop=(not diag)),
                    reads=(kak, qak), writes=(pk,))
                if diag:
                    P.op("pe", lambda e, ps=ps: e.matmul(ps[:, 0:128], idb[:, :], mneg[:, :], start=False, stop=True),
                         reads=("idb", "mneg"), writes=(pk,))
                pt, ptk = ptp[kb % 4]
                P.op("act", lambda e, ps=ps, pt=pt, n=n: e.activation(out=pt[:, 0:n], in_=ps[:, 0:n], func=AF.Exp, scale=1.0 / SQD),
                     reads=(pk,), writes=(ptk,))
                P.op("pe", lambda e, ops_=ops_, vtok=vtok, kb=kb, pt=pt, off=off, n=n, nkb=nkb: e.matmul(
                    ops_[:, off:512], vtok[:, kb, :], pt[:, 0:n], start=(kb == 0), stop=(kb == nkb - 1)),
                    reads=(vtk, ptk), writes=(opk,))
                P.op("pe", lambda e, lps=lps, pt=pt, off=off, n=n, kb=kb, nkb=nkb: e.matmul(
                    lps[:, off:512], ones_bf[:, :], pt[:, 0:n], start=(kb == 0), stop=(kb == nkb - 1)),
                    reads=("ones_bf", ptk), writes=(lpk,))
            rc, rck = rcp[qb % 2]
            yt, yk = yop[qb % 2]
            P.op("dve", lambda e, rc=rc, lps=lps: e.reciprocal(out=rc[:, :], in_=lps[:, :]), reads=(lpk,), writes=(rck,))
            P.op("dve", lambda e, yt=yt, ops_=ops_, rc=rc: e.tensor_tensor(out=yt[:, :], in0=ops_[:, :], in1=rc[:, :], op=ALU.mult),
                 reads=(opk, rck), writes=(yk,))
            P.dma("sp", mixo[PC + h * 128:PC + (h + 1) * 128, t0:t0 + 512], yt[:, :], yk, reads=(yk,))
    P.run()


def build_gdn(nc, g, cfg, projT, qkd, kvd, gconv, galog, gdtb, gnrep, cmask, mixo, c_gq, c_gk, c_gv, c_gz, c_sc, stop=9):
    T, HC, PC, AC = cfg.T, cfg.HC, cfg.PC, cfg.AC
    NCH = T // 128
    scd = nc.dram_tensor("gdn_scd", [128, NCH, 6 * HC], F32, kind="Internal").ap()
    P = Prog(nc, g)
    P.alloc_psum()
    idf = P.sbuf("idf", [128, 128], F32)
    triu = P.sbuf("triu", [128, 128], F32)
    onesf = P.sbuf("onesf", [128, 128], F32)
    P.dma("sp", idf[:, :], cmask[0], "idf", writes=("idf",))
    P.dma("sp", triu[:, :], cmask[1], "triu", writes=("triu",))
    P.op("pool", lambda e: e.memset(onesf[:, :], 1.0), writes=("onesf",))
    bsb = P.sbuf("bsb", [HC, T], F32)
    asb = P.sbuf("asb", [HC, T], F32)
    al = P.sbuf("al", [HC, 1], F32)
    db = P.sbuf("db", [HC, 1], F32)
    P.dma("sp", bsb[:, :], projT[c_sc + HC:c_sc + 2 * HC, :], "bsb", writes=("bsb",))
    P.dma("sp", asb[:, :], projT[c_sc + 2 * HC:c_sc + 3 * HC, :], "asb", writes=("asb",))
    P.dma("sp", al[:, :], galog, "al", writes=("al",))
    P.dma("sp", db[:, :], gdtb, "db", writes=("db",))
    P.op("act", lambda e: e.activation(out=bsb[:, :], in_=bsb[:, :], func=AF.Exp, scale=-1.0), reads=("bsb",), writes=("bsb",))
    P.op("dve", lambda e: e.tensor_scalar(out=bsb[:, :], in0=bsb[:, :], scalar1=1.0, scalar2=None, op0=ALU.add), reads=("bsb",), writes=("bsb",))
    P.op("dve", lambda e: e.reciprocal(out=bsb[:, :], in_=bsb[:, :]), reads=("bsb",), writes=("bsb",))
    P.op("act", lambda e: e.activation(out=al[:, :], in_=al[:, :], func=AF.Exp), reads=("al",), writes=("al",))
    P.op("act", lambda e: e.activation(out=asb[:, :], in_=asb[:, :], func=AF.Exp, bias=db[:, 0:1]), reads=("asb", "db"), writes=("asb",))
    P.op("act", lambda e: e.activation(out=asb[:, :], in_=asb[:, :], func=AF.Ln, bias=1.0), reads=("asb",), writes=("asb",))
    P.op("dve", lambda e: e.tensor_scalar(out=asb[:, :], in0=asb[:, :], scalar1=al[:, 0:1], scalar2=-1.0, op0=ALU.mult, op1=ALU.mult),
         reads=("asb", "al"), writes=("asb",))
    sc = P.sbuf("sc", [128, NCH, 2 * HC], F32)
    gam = P.sbuf("gam", [128, NCH, 2 * HC], F32)
    for n in range(NCH):
        ps, pk = P.ps()
        P.op("pe", lambda e, ps=ps, n=n: e.transpose(ps[:, 0:HC], bsb[:, n * 128:(n + 1) * 128], idf[0:HC, 0:HC]),
             reads=("bsb", "idf"), writes=(pk,))
        P.op("pe", lambda e, ps=ps, n=n: e.transpose(ps[:, HC:2 * HC], asb[:, n * 128:(n + 1) * 128], idf[0:HC, 0:HC]),
             reads=("asb", "idf"), writes=(pk,))
        P.op("act", lambda e, ps=ps, n=n: e.activation(out=sc[:, n, :], in_=ps[:, 0:2 * HC], func=AF.Copy), reads=(pk,), writes=("sc",))
        ps2, pk2 = P.ps()
        P.op("pe", lambda e, ps2=ps2, n=n: e.matmul(ps2[:, 0:HC], triu[:, :], sc[:, n, HC:2 * HC], start=True, stop=True),
             reads=("triu", "sc"), writes=(pk2,))
        P.op("pe", lambda e, ps2=ps2, n=n: e.matmul(ps2[:, HC:2 * HC], onesf[:, :], sc[:, n, HC:2 * HC], start=True, stop=True),
             reads=("onesf", "sc"), writes=(pk2,))
        P.op("dve", lambda e, ps2=ps2, n=n: e.tensor_copy(out=gam[:, n, :], in_=ps2[:, 0:2 * HC]), reads=(pk2,), writes=("gam",))
    der = P.sbuf("der", [128, NCH, 6 * HC], F32)
    P.op("act", lambda e: e.activation(out=der[:, :, 0:2 * HC], in_=gam[:, :, :], func=AF.Exp), reads=("gam",), writes=("der",))
    P.op("dve", lambda e: e.tensor_tensor(out=der[:, :, 2 * HC:3 * HC], in0=gam[:, :, HC:2 * HC], in1=gam[:, :, 0:HC], op=ALU.subtract),
         reads=("gam",), writes=("der",))
    P.op("act", lambda e: e.activation(out=der[:, :, 2 * HC:3 * HC], in_=der[:, :, 2 * HC:3 * HC], func=AF.Exp), reads=("der",), writes=("der",))
    P.op("dve", lambda e: e.tensor_tensor(out=der[:, :, 3 * HC:4 * HC], in0=der[:, :, 0:HC], in1=sc[:, :, 0:HC], op=ALU.mult),
         reads=("der", "sc"), writes=("der",))
    P.op("dve", lambda e: e.tensor_copy(out=der[:, :, 4 * HC:6 * HC], in_=sc[:, :, :]), reads=("sc",), writes=("der",))
    P.dma("sp", scd, der[:, :, :], "der", reads=("der",))
    P.run()
    if stop < 6:
        return
    P = Prog(nc, g)
    P.alloc_psum()
    ones_bf = consts(P, cfg)
    e128 = P.sbuf("e128", [128, 16], F32)
    P.op("pool", lambda e: e.memset(e128[:, :], 128.0 * 1e-6), writes=("e128",))
    cw = P.sbuf("gcw", [128, 3 * HC, 4], F32)
    P.dma("sp", cw[:, :, :], gconv, "gcw", writes=("gcw",))
    xp = P.pool_("gx", [128, 3 + T], F32, 2)
    cp = P.pool_("gc", [128, T], F32, 2)
    sqp = P.pool_("gsq", [128, T], BF16, 1)
    rp = P.pool_("gr", [128, 512], F32, 2)
    ob = P.pool_("gob", [128, T], BF16, 2)
    for (t_, k_) in xp:
        P.op("pool", lambda e, t_=t_: e.memset(t_[:, 0:3], 0.0), writes=(k_,))
    for j in range(3 * HC):
        kind, h = j // HC, j % HC
        base = (c_gq, c_gk, c_gv)[kind] + h * 128
        xt, xk = xp[j % 2]
        ct, ck = cp[j % 2]
        P.dma("sp", xt[:, 3:3 + T], projT[base:base + 128, :], xk, writes=(xk,))
        P.op("dve", lambda e, ct=ct, xt=xt, j=j: e.tensor_scalar(out=ct[:, :], in0=xt[:, 0:T], scalar1=cw[:, j, 0:1], scalar2=None, op0=ALU.mult),
             reads=(xk, "gcw"), writes=(ck,))
        for i in range(1, 4):
            P.op("dve", lambda e, ct=ct, xt=xt, j=j, i=i: e.scalar_tensor_tensor(
                out=ct[:, :], in0=xt[:, i:i + T], scalar=cw[:, j, i:i + 1], in1=ct[:, :], op0=ALU.mult, op1=ALU.add),
                reads=(xk, ck, "gcw"), writes=(ck,))
        P.op("act", lambda e, ct=ct: e.activation(out=ct[:, :], in_=ct[:, :], func=AF.Silu), reads=(ck,), writes=(ck,))
        if kind < 2:
            sq, sqk = sqp[0]
            P.op("act", lambda e, sq=sq, ct=ct: e.activation(out=sq[:, :], in_=ct[:, :], func=AF.Square), reads=(ck,), writes=(sqk,))
            for ti, (c0, n) in enumerate(col_tiles(0, T)):
                ps, pk = P.ps()
                rt, rk = rp[ti % 2]
                P.op("pe", lambda e, ps=ps, sq=sq, c0=c0, n=n: e.matmul(ps[:, 0:n], ones_bf[:, :], sq[:, c0:c0 + n], start=True, stop=True),
                     reads=(sqk, "ones_bf"), writes=(pk,))
                if kind == 0:
                    P.op("act", lambda e, rt=rt, ps=ps, n=n: e.activation(out=rt[:, 0:n], in_=ps[:, 0:n], func=AF.Ln, scale=128.0, bias=e128[:, 0:1]),
                         reads=(pk, "e128"), writes=(rk,))
                else:
                    P.op("act", lambda e, rt=rt, ps=ps, n=n: e.activation(out=rt[:, 0:n], in_=ps[:, 0:n], func=AF.Ln, bias=P.eps_ap),
                         reads=(pk, "eps_t"), writes=(rk,))
                P.op("act", lambda e, rt=rt, n=n: e.activation(out=rt[:, 0:n], in_=rt[:, 0:n], func=AF.Exp, scale=-0.5), reads=(rk,), writes=(rk,))
                P.op("dve", lambda e, ct=ct, rt=rt, c0=c0, n=n: e.tensor_tensor(out=ct[:, c0:c0 + n], in0=ct[:, c0:c0 + n], in1=rt[:, 0:n], op=ALU.mult),
                     reads=(ck, rk), writes=(ck,))
            ot, ok = ob[j % 2]
            P.op("dve", lambda e, ot=ot, ct=ct: e.tensor_copy(out=ot[:, :], in_=ct[:, :]), reads=(ck,), writes=(ok,))
            P.dma("sp", qkd[:, j, :], ot[:, :], ok, reads=(ok,))
        if kind >= 1:
            P.dma("sp", kvd[:, j - HC, :], ct[:, :], ck, reads=(ck,))
    P.run()
    if stop < 7:
        return
    P = Prog(nc, g)
    P.alloc_psum()
    ones_bf = consts(P, cfg)
    cm = P.sbuf("cm", [128, 5, 128], F32)
    for i in range(5):
        P.dma("sp", cm[:, i, :], cmask[i], "cm", writes=("cm",))
    idf, triu, mgt, mincl, mnstr = (cm[:, i, :] for i in range(5))
    gn = P.sbuf("gn", [128, 128], F32)
    P.dma("sp", gn[:, :], gnrep, "gn", writes=("gn",))
    der = P.sbuf("der", [128, NCH, 6 * HC], F32)
    P.dma("sp", der[:, :, :], scd, "der", writes=("der",))
    tl = {}

    def TT(name, h, dtype=F32, shape=(128, 128)):
        k = f"{name}_{h}"
        if k not in tl:
            tl[k] = (P.sbuf(k, list(shape), dtype), k)
        return tl[k]
    qkp = P.pool_("qk", [128, 2 * HC, 128], BF16, 2)
    kvp = P.pool_("kv", [128, 2 * HC, 128], F32, 2)
    zp = P.pool_("z", [128, HC, 128], F32, 2)
    yp = P.pool_("y", [128, HC, 128], BF16, 2)
    zview = projT[c_gz:c_gz + HC * 128, :].rearrange("(h p) t -> p h t", p=128)
    oview = mixo[PC + AC:PC + 2 * AC, :].rearrange("(h p) t -> p h t", p=128)
    for h in range(HC):
        S32, S32k = TT("S32", h)
        Sb, Sbk = TT("Sb", h, BF16)
        P.op("pool", lambda e, S32=S32: e.memset(S32[:, :], 0.0), writes=(S32k,))
        P.op("pool", lambda e, Sb=Sb: e.memset(Sb[:, :], 0.0), writes=(Sbk,))

    def mm(ps, pk, lhsT, rhs, reads):
        P.op("pe", lambda e: e.matmul(ps[:, 0:128], lhsT, rhs, start=True, stop=True), reads=reads, writes=(pk,))

    def tr(ps, pk, src, reads):
        P.op("pe", lambda e: e.transpose(ps[:, 0:128], src, idf), reads=tuple(reads) + ("cm",), writes=(pk,))

    def acopy(out, ok, ps, pk, scale=None, extra=()):
        if scale is None:
            P.op("act", lambda e: e.activation(out=out, in_=ps[:, 0:128], func=AF.Copy), reads=(pk,) + tuple(extra), writes=(ok,))
        else:
            P.op("act", lambda e: e.activation(out=out, in_=ps[:, 0:128], func=AF.Copy, scale=scale), reads=(pk, "der") + tuple(extra), writes=(ok,))

    import os
    SEC = int(os.environ.get('GDN_SEC', '99'))
    for n in range(NCH):
        cs = slice(n * 128, (n + 1) * 128)
        qk, qkk = qkp[n % 2]; kv, kvk = kvp[n % 2]; zt, zk = zp[n % 2]; yt, yk = yp[n % 2]
        P.dma("sp", qk[:, :, :], qkd[:, :, cs], qkk, writes=(qkk,))
        P.dma("sp", kv[:, :, :], kvd[:, :, cs], kvk, writes=(kvk,))
        P.dma("sp", zt[:, :, :], zview[:, :, cs], zk, writes=(zk,))
        P.op("act", lambda e, zt=zt: e.activation(out=zt[:, :, :], in_=zt[:, :, :], func=AF.Silu), reads=(zk,), writes=(zk,))
        for h in range(HC):
            sv = lambda slot, n=n, h=h: der[:, n, slot * HC + h:slot * HC + h + 1]
            kT, qT = qk[:, HC + h, :], qk[:, h, :]
            k32T, v32T = kv[:, h, :], kv[:, HC + h, :]
            ps, pk = P.ps(); tr(ps, pk, k32T, (kvk,))
            kdec, kdeck = TT("kdec", h, BF16); acopy(kdec[:, :], kdeck, ps, pk, scale=sv(2))
            kbg, kbgk = TT("kbg", h, BF16)
            P.op("dve", lambda e, kbg=kbg, ps=ps, sv=sv: e.tensor_scalar(out=kbg[:, :], in0=ps[:, 0:128], scalar1=sv(3), scalar2=None, op0=ALU.mult),
                 reads=(pk, "der"), writes=(kbgk,))
            kbt, kbtk = TT("kbt", h)
            P.op("dve", lambda e, kbt=kbt, ps=ps, sv=sv: e.tensor_scalar(out=kbt[:, :], in0=ps[:, 0:128], scalar1=sv(4), scalar2=None, op0=ALU.mult),
                 reads=(pk, "der"), writes=(kbtk,))
            ps, pk = P.ps(); tr(ps, pk, v32T, (kvk,))
            vb, vbk = TT("vb", h, BF16); acopy(vb[:, :], vbk, ps, pk, scale=sv(4))
            ps, pk = P.ps(); tr(ps, pk, kbt[:, :], (kbtk,))
            kbT, kbTk = TT("kbT", h, BF16); acopy(kbT[:, :], kbTk, ps, pk)
            if SEC < 4:
                continue
            R1, R1k = TT("R1", h)
            P.op("pool", lambda e, R1=R1, sv=sv: e.tensor_scalar(out=R1[:, :], in0=mgt, scalar1=sv(5), scalar2=None, op0=ALU.mult),
                 reads=("cm", "der"), writes=(R1k,))
            ps, pk = P.ps(); mm(ps, pk, R1[:, :], triu, (R1k, "cm"))
            ET, ETk = TT("ET", h)
            P.op("act", lambda e, ET=ET, ps=ps: e.activation(out=ET[:, :], in_=ps[:, 0:128], func=AF.Exp), reads=(pk,), writes=(ETk,))
            ETm, ETmk = TT("ETm", h); nETs, nETsk = TT("nETs", h)
            P.op("pool", lambda e, ETm=ETm, ET=ET: e.tensor_tensor(out=ETm[:, :], in0=ET[:, :], in1=mincl, op=ALU.mult), reads=(ETk, "cm"), writes=(ETmk,))
            P.op("pool", lambda e, nETs=nETs, ET=ET: e.tensor_tensor(out=nETs[:, :], in0=ET[:, :], in1=mnstr, op=ALU.mult), reads=(ETk, "cm"), writes=(nETsk,))
            if SEC < 5:
                continue
            ps, pk = P.ps(); mm(ps, pk, kT, kbT[:, :], (qkk, kbTk))
            XA, XAk = TT("XA", h); XB, XBk = TT("XB", h); XTA, XTAk = TT("XTA", h); XTB, XTBk = TT("XTB", h)
            P.op("dve", lambda e, XA=XA, ps=ps, nETs=nETs: e.tensor_tensor(out=XA[:, :], in0=ps[:, 0:128], in1=nETs[:, :], op=ALU.mult),
                 reads=(pk, nETsk), writes=(XAk,))
            ps, pk = P.ps(); mm(ps, pk, kT, qT, (qkk,))
            itT, itTk = TT("itT", h, BF16)
            P.op("dve", lambda e, itT=itT, ps=ps, ETm=ETm: e.tensor_tensor(out=itT[:, :], in0=ps[:, 0:128], in1=ETm[:, :], op=ALU.mult),
                 reads=(pk, ETmk), writes=(itTk,))
            ps, pk = P.ps(); tr(ps, pk, XA[:, :], (XAk,))
            acopy(XTA[:, :], XTAk, ps, pk)
            if SEC < 6:
                continue
            R, Rk = TT("R", h)
            P.op("pool", lambda e, R=R, XA=XA: e.tensor_tensor(out=R[:, :], in0=XA[:, :], in1=idf, op=ALU.add), reads=(XAk, "cm"), writes=(Rk,))
            X, Xk, XT_, XTk = XA, XAk, XTA, XTAk
            Xn, Xnk, XTn, XTnk = XB, XBk, XTB, XTBk
            for k in range(1, 7):
                if k < 6:
                    ps, pk = P.ps(); mm(ps, pk, XT_[:, :], X[:, :], (XTk, Xk))
                    acopy(Xn[:, :], Xnk, ps, pk)
                ps, pk = P.ps(); mm(ps, pk, X[:, :], XT_[:, :], (XTk, Xk))
                P.op("dve", lambda e, XTn=XTn, ps=ps: e.tensor_copy(out=XTn[:, :], in_=ps[:, 0:128]), reads=(pk,), writes=(XTnk,))
                X, Xk, XT_, XTk, Xn, Xnk, XTn, XTnk = Xn, Xnk, XTn, XTnk, X, Xk, XT_, XTk
                ps, pk = P.ps(); mm(ps, pk, XT_[:, :], R[:, :], (XTk, Rk))
                P.op("dve", lambda e, R=R, ps=ps: e.tensor_tensor(out=R[:, :], in0=ps[:, 0:128], in1=R[:, :], op=ALU.add), reads=(pk, Rk), writes=(Rk,))
            Rb, Rbk = TT("Rb", h, BF16)
            P.op("act", lambda e, Rb=Rb, R=R: e.activation(out=Rb[:, :], in_=R[:, :], func=AF.Copy), reads=(Rk,), writes=(Rbk,))
            if SEC < 7:
                continue
            ps, pk = P.ps(); mm(ps, pk, Rb[:, :], vb[:, :], (Rbk, vbk))
            u32, u32k = TT("u32", h); acopy(u32[:, :], u32k, ps, pk)
            ps, pk = P.ps(); mm(ps, pk, kbg[:, :], Rb[:, :], (kbgk, Rbk))
            wT, wTk = TT("wT", h, BF16)
            P.op("dve", lambda e, wT=wT, ps=ps: e.tensor_copy(out=wT[:, :], in_=ps[:, 0:128]), reads=(pk,), writes=(wTk,))
            if SEC < 8:
                continue
            S32, S32k = TT("S32", h); Sb, Sbk = TT("Sb", h, BF16)
            ps, pk = P.ps(); mm(ps, pk, wT[:, :], Sb[:, :], (wTk, Sbk))
            vn, vnk = TT("vn", h, BF16)
            P.op("dve", lambda e, vn=vn, u32=u32, ps=ps: e.tensor_tensor(out=vn[:, :], in0=u32[:, :], in1=ps[:, 0:128], op=ALU.subtract),
                 reads=(pk, u32k), writes=(vnk,))
            ps, pk = P.ps(); mm(ps, pk, qT, Sb[:, :], (qkk, Sbk))
            o1, o1k = TT("o1", h); acopy(o1[:, :], o1k, ps, pk, scale=sv(0))
            ps, pk = P.ps(); mm(ps, pk, itT[:, :], vn[:, :], (itTk, vnk))
            o, ok_ = TT("o", h)
            P.op("dve", lambda e, o=o, ps=ps, o1=o1: e.tensor_tensor(out=o[:, :], in0=ps[:, 0:128], in1=o1[:, :], op=ALU.add),
                 reads=(pk, o1k), writes=(ok_,))
            ps, pk = P.ps(); mm(ps, pk, kdec[:, :], vn[:, :], (kdeck, vnk))
            P.op("dve", lambda e, S32=S32, ps=ps, sv=sv: e.scalar_tensor_tensor(out=S32[:, :], in0=S32[:, :], scalar=sv(1), in1=ps[:, 0:128],
                                                                                op0=ALU.mult, op1=ALU.add), reads=(pk, S32k, "der"), writes=(S32k,))
            P.op("act", lambda e, Sb=Sb, S32=S32: e.activation(out=Sb[:, :], in_=S32[:, :], func=AF.Copy), reads=(S32k,), writes=(Sbk,))
            if SEC < 9:
                continue
            ps, pk = P.ps(); tr(ps, pk, o[:, :], (ok_,))
            oT, oTk = TT("oT", h)
            acopy(oT[:, :], oTk, ps, pk)
            sqb, sqbk = TT("sqb", h, BF16)
            P.op("act", lambda e, sqb=sqb, oT=oT: e.activation(out=sqb[:, :], in_=oT[:, :], func=AF.Square), reads=(oTk,), writes=(sqbk,))
            ps, pk = P.ps(); mm(ps, pk, ones_bf[:, :], sqb[:, :], (sqbk, "ones_bf"))
            rt, rtk = TT("rt", h)
            P.op("act", lambda e, rt=rt, ps=ps: e.activation(out=rt[:, :], in_=ps[:, 0:128], func=AF.Ln, scale=1.0 / 128.0, bias=P.eps_ap),
                 reads=(pk, "eps_t"), writes=(rtk,))
            P.op("act", lambda e, rt=rt: e.activation(out=rt[:, :], in_=rt[:, :], func=AF.Exp, scale=-0.5), reads=(rtk,), writes=(rtk,))
            P.op("dve", lambda e, oT=oT, rt=rt: e.scalar_tensor_tensor(out=oT[:, :], in0=oT[:, :], scalar=gn[:, 0:1], in1=rt[:, :],
                                                                        op0=ALU.mult, op1=ALU.mult), reads=(oTk, rtk, "gn"), writes=(oTk,))
            P.op("dve", lambda e, yt=yt, h=h, oT=oT, zt=zt: e.tensor_tensor(out=yt[:, h, :], in0=oT[:, :], in1=zt[:, h, :], op=ALU.mult),
                 reads=(oTk, zk), writes=(yk,))
        P.dma("sp", oview[:, :, cs], yt[:, :, :], yk, reads=(yk,))
    P.run()


POOL_WINDOWS = (2, 4, 8, 16)


def tile_w(w):
    Kd, N = w.shape
    return np.ascontiguousarray(w.reshape(Kd // 128, 128, N // 128, 128).transpose(2, 1, 0, 3))


def const_masks():
    i = np.arange(128)
    ident = np.eye(128, dtype=np.float32)
    triu = (i[:, None] <= i[None, :]).astype(np.float32)
    mgt = (i[:, None] > i[None, :]).astype(np.float32)
    incl = (i[:, None] <= i[None, :]).astype(np.float32)
    nstr = -(i[:, None] < i[None, :]).astype(np.float32)
    cm = np.stack([ident, triu, mgt, incl, nstr]).astype(np.float32)
    mneg = np.where(i[:, None] > i[None, :], -30000.0, 0.0).astype(np.float32)
    cmb = np.stack([ident, mneg]).astype(NPBF)
    return cm, cmb


def mixer_inputs(cfg, l, hh, xb, inp):
    D, T, KC, GD, HC, AC, H = cfg.D, cfg.T, cfg.KC, cfg.GD, cfg.HC, cfg.AC, cfg.H
    POOL, ATT, GC = cfg.POOL, cfg.ATT, GD // 128
    w = inp["w_in"][l]
    b_fox = POOL
    b_ff = POOL + 3 * ATT
    b_g = b_ff + H
    b_z = b_g + 3 * ATT
    b_b = b_z + ATT
    b_a = b_b + H
    r = lambda s, n: np.arange(s, s + n)
    cols = np.concatenate([
        r(2 * hh * GD, 2 * GD),
        r(b_fox + hh * AC, AC), r(b_fox + ATT + hh * AC, AC), r(b_fox + 2 * ATT + hh * AC, AC),
        r(b_g + hh * AC, AC), r(b_g + ATT + hh * AC, AC), r(b_g + 2 * ATT + hh * AC, AC),
        r(b_z + hh * AC, AC),
        r(b_ff + hh * HC, HC), r(b_b + hh * HC, HC), r(b_a + hh * HC, HC)])
    wsel = np.zeros((D, cfg.NCOL), np.float32)
    wsel[:, :cols.size] = w[:, cols]
    pw = np.zeros((2 * GC, 128, GC, 128), np.float32)
    psc = np.zeros((128, 2 * GC), np.float32)
    coef = np.zeros((2, 128, 4, T), np.float32)
    pos = np.arange(1, T + 1, dtype=np.float32)
    for gi in range(2):
        gidx = 2 * hh + gi
        wg = inp["pool_w"][l][gidx]
        for ec in range(GC):
            pw[gi * GC + ec] = wg[:, ec * 128:(ec + 1) * 128].reshape(GC, 128, 128).transpose(1, 0, 2)
            psc[:, gi * GC + ec] = inp["pool_scale"][l][gidx * GD + ec * 128:gidx * GD + (ec + 1) * 128]
        coef[gi, :, gidx, :] = (1.0 / np.minimum(pos, float(POOL_WINDOWS[gidx])))[None, :]
    gdn = ATT
    cw = inp["gdn_conv_w"][l]
    gconv = np.zeros((128, 3 * HC, 4), np.float32)
    for kind in range(3):
        for h in range(HC):
            ch0 = kind * gdn + (hh * HC + h) * 128
            gconv[:, kind * HC + h, :] = cw[:, ch0:ch0 + 128].T
    cm, cmb = const_masks()
    return {
        "xT": np.ascontiguousarray(xb.T), "gain": np.ascontiguousarray(inp["norm_mix_gain"][l].reshape(KC, 128).T),
        "win": tile_w(wsel), "poolw": pw, "poolsc": psc, "poolcoef": coef,
        "foxb": np.ascontiguousarray(inp["fox_f_bias"][l][hh * HC:(hh + 1) * HC].reshape(HC, 1)),
        "gconv": gconv,
        "galog": np.ascontiguousarray(inp["gdn_A_log"][l][hh * HC:(hh + 1) * HC].reshape(HC, 1)),
        "gdtb": np.ascontiguousarray(inp["gdn_dt_bias"][l][hh * HC:(hh + 1) * HC].reshape(HC, 1)),
        "gnrep": np.ascontiguousarray(np.tile(inp["gdn_norm_gain"][l][:, None], (1, 128))),
        "cmask": cm, "cmaskb": cmb,
    }


def assemble_mix(cfg, outs):
    D, T, GD, AC, POOL, ATT, PC = cfg.D, cfg.T, cfg.GD, cfg.AC, cfg.POOL, cfg.ATT, cfg.PC
    m = np.zeros((D, T), NPBF)
    for hh in range(2):
        o = outs[hh]
        m[2 * hh * GD:(2 * hh + 2) * GD] = o[:PC]
        m[POOL + hh * AC:POOL + (hh + 1) * AC] = o[PC:PC + AC]
        m[POOL + ATT + hh * AC:POOL + ATT + (hh + 1) * AC] = o[PC + AC:PC + 2 * AC]
    return m


def ffn_weights(cfg, l, inp):
    KC, FC = cfg.KC, cfg.FC
    gains = np.stack([inp["norm_ffn_gain"][l].reshape(KC, 128).T, inp["final_norm_gain"].reshape(KC, 128).T], axis=1)
    convw = np.ascontiguousarray(inp["ffn_conv_w"][l].reshape(3, 2 * FC, 128).transpose(2, 1, 0))
    return {"wo": tile_w(inp["w_o"][l]), "wup": tile_w(inp["w_up"][l]), "wdn": tile_w(inp["w_down"][l]),
            "gains": np.ascontiguousarray(gains.astype(np.float32)), "convw": convw}


def halo_cols(a, hh, TH):
    out = np.zeros((a.shape[0], TH + 2), a.dtype)
    if hh == 0:
        out[:, 2:] = a[:, :TH]
    else:
        out[:, :] = a[:, TH - 2:2 * TH]
    return out


_NC_CACHE = {}


def run_model(cfg, inp, depth):
    NB, T, D, TH = cfg.NB, cfg.T, cfg.D, cfg.TH
    ncores = 2 * NB
    x = np.asarray(inp["x"], np.float32)
    xT = [np.ascontiguousarray(x[b].T) for b in range(NB)]
    key = (cfg.D, cfg.T, cfg.FF)
    if ("m",) + key not in _NC_CACHE:
        _NC_CACHE[("m",) + key] = build_mixer_launch(cfg)
    for l in range(depth):
        ncm = _NC_CACHE[("m",) + key]
        maps = []
        for c in range(ncores):
            b, hh = c // 2, c % 2
            maps.append(mixer_inputs(cfg, l, hh, xT[b].T, inp))
        res = run_bass_kernel_spmd(ncm, maps, core_ids=list(range(ncores)))
        mix = [assemble_mix(cfg, [res.results[2 * b]["mixo"], res.results[2 * b + 1]["mixo"]]) for b in range(NB)]
        final = (l == depth - 1)
        fk = ("f", final) + key
        if fk not in _NC_CACHE:
            _NC_CACHE[fk] = build_ffn_launch(cfg, final)
        fw = ffn_weights(cfg, l, inp)
        maps = []
        for c in range(ncores):
            b, hh = c // 2, c % 2
            m = dict(fw)
            m["xT"] = halo_cols(xT[b], hh, TH)
            m["mixT"] = halo_cols(mix[b], hh, TH)
            maps.append(m)
        res = run_bass_kernel_spmd(_NC_CACHE[fk], maps, core_ids=list(range(ncores)))
        xT = [np.concatenate([res.results[2 * b]["out"], res.results[2 * b + 1]["out"]], axis=1) for b in range(NB)]
    return np.stack([np.ascontiguousarray(xT[b].T) for b in range(NB)]).astype(np.float32)


def kernel(**inputs):
    cfg = Cfg()
    inp = {k: np.asarray(v) for k, v in inputs.items()}
    return run_model(cfg, inp, 2)
```
